# Optimizing a Trainium2 kernel written in Bass

```python
import jax, jax.numpy as jnp
from jax import lax
import numpy as np

D_MODEL = 2048
BATCH = 4
SEQ = 4096
DEPTH = 4

CHUNK = 64
Q_BLOCK = 128
EPS = 1e-6
NEG_INF = -1e30
ROPE_BASE = 10000.0

RET_HEADS = 8
RET_DK = 128
RET_DV = 256
RET_QK_W = RET_HEADS * RET_DK
RET_V_W = RET_HEADS * RET_DV

MLA_HEADS = 16
MLA_Q_RANK = 512
MLA_KV_RANK = 512
MLA_NOPE = 128
MLA_ROPE = 64
MLA_DV = 128
MLA_V_W = MLA_HEADS * MLA_DV

N_BRANCH = 2
IN_SPLITS = (RET_QK_W, RET_QK_W, RET_V_W, RET_V_W, MLA_Q_RANK, MLA_KV_RANK, MLA_ROPE, MLA_V_W, N_BRANCH * D_MODEL)
D_IN = 2 * RET_QK_W + 2 * RET_V_W + MLA_Q_RANK + MLA_KV_RANK + MLA_ROPE + MLA_V_W + N_BRANCH * D_MODEL

kernel_name = "hybrid_retention_mla_adaln_trunk"


def rms_norm(x, g):
    xf = x.astype(jnp.float32)
    y = xf * lax.rsqrt(jnp.mean(xf * xf, axis=-1, keepdims=True) + EPS)
    return (y * g.astype(jnp.float32)).astype(x.dtype)


def rope_tables(positions, dim):
    inv = 1.0 / (ROPE_BASE ** (jnp.arange(0, dim, 2, dtype=jnp.float32) / dim))
    ang = positions.astype(jnp.float32)[..., None] * inv
    return jnp.cos(ang), jnp.sin(ang)


def apply_rope(x, cos, sin):
    half = x.shape[-1] // 2
    x1, x2 = x[..., :half], x[..., half:]
    cos = cos.astype(x.dtype)
    sin = sin.astype(x.dtype)
    return jnp.concatenate([x1 * cos - x2 * sin, x1 * sin + x2 * cos], axis=-1)


def retention(q, k, v):
    B, S, H, dk = q.shape
    dv = v.shape[-1]
    nc = S // CHUNK
    f32 = jnp.float32
    log_gamma = jnp.log(1.0 - 2.0 ** (-5.0 - jnp.arange(H, dtype=f32)))
    qf = q.astype(f32).reshape(B, nc, CHUNK, H, dk)
    kf = (k.astype(f32) * (dk ** -0.5)).reshape(B, nc, CHUNK, H, dk)
    vf = v.astype(f32).reshape(B, nc, CHUNK, H, dv)
    idx = jnp.arange(CHUNK, dtype=f32)
    dmat = jnp.exp(jnp.abs(idx[:, None] - idx[None, :])[None] * log_gamma[:, None, None])
    scores = jnp.einsum('bnihd,bnjhd->bnhij', qf, kf) * dmat[None, None]
    o_intra = jnp.einsum('bnhij,bnjhe->bnihe', scores, vf)
    xi = jnp.exp((idx + 1.0)[:, None] * log_gamma[None, :])
    zeta = jnp.exp((CHUNK - 1.0 - idx)[:, None] * log_gamma[None, :])
    decay_chunk = jnp.exp(CHUNK * log_gamma)
    q_x = (qf * xi[None, None, :, :, None]).transpose(1, 0, 2, 3, 4)
    k_z = (kf * zeta[None, None, :, :, None]).transpose(1, 0, 2, 3, 4)
    v_t = vf.transpose(1, 0, 2, 3, 4)

    def step(state, inp):
        qc, kc, vc = inp
        o = jnp.einsum('bihd,bhde->bihe', qc, state)
        state = state * decay_chunk[None, :, None, None] + jnp.einsum('bjhd,bjhe->bhde', kc, vc)
        return state, o

    init = jnp.zeros((B, H, dk, dv), f32)
    _, o_cross = lax.scan(step, init, (q_x, k_z, v_t))
    o = o_intra + o_cross.transpose(1, 0, 2, 3, 4)
    return o.reshape(B, S, H, dv)


def head_group_norm(o):
    mu = jnp.mean(o, axis=-1, keepdims=True)
    var = jnp.mean(jnp.square(o - mu), axis=-1, keepdims=True)
    return (o - mu) * lax.rsqrt(var + EPS)


def mla_attention(q_nope, q_rope, k_nope, k_rope, v):
    B, S, H, _ = q_nope.shape
    nb = S // Q_BLOCK
    scale = (MLA_NOPE + MLA_ROPE) ** -0.5
    key_chunk = jnp.arange(S) // CHUNK

    def block(i):
        qs = i * Q_BLOCK
        qn = lax.dynamic_slice_in_dim(q_nope, qs, Q_BLOCK, axis=1)
        qr = lax.dynamic_slice_in_dim(q_rope, qs, Q_BLOCK, axis=1)
        s = (jnp.einsum('bqhd,bkhd->bhqk', qn, k_nope)
             + jnp.einsum('bqhd,bkd->bhqk', qr, k_rope)).astype(jnp.float32) * scale
        q_chunk = (qs + jnp.arange(Q_BLOCK)) // CHUNK
        mask = key_chunk[None, :] <= q_chunk[:, None]
        s = jnp.where(mask[None, None], s, NEG_INF)
        p = jax.nn.softmax(s, axis=-1)
        return jnp.einsum('bhqk,bkhd->bqhd', p.astype(v.dtype), v)

    out = lax.map(block, jnp.arange(nb))
    return out.transpose(1, 0, 2, 3, 4).reshape(B, S, H * v.shape[-1])


def hybrid_layer(x, c_act, cos_r, sin_r, cos_m, sin_m,
                 w_mod, b_mod, g_norm, w_in, g_cq, g_ckv, w_uq, w_ukv,
                 w_ret_proj, w_mla_proj, w_out):
    B, S, D = x.shape
    mod = c_act @ w_mod + b_mod
    shift, scale, gate = jnp.split(mod, 3, axis=-1)
    h = rms_norm(x, g_norm) * (1.0 + scale[:, None, :]) + shift[:, None, :]

    proj = h @ w_in
    points = np.cumsum(IN_SPLITS)[:-1].tolist()
    rq, rk, rv, rg, cq, ckv, kr, mg, bg = jnp.split(proj, points, axis=-1)

    rq = apply_rope(rq.reshape(B, S, RET_HEADS, RET_DK), cos_r[:, :, None], sin_r[:, :, None])
    rk = apply_rope(rk.reshape(B, S, RET_HEADS, RET_DK), cos_r[:, :, None], sin_r[:, :, None])
    rv = rv.reshape(B, S, RET_HEADS, RET_DV)
    o_ret = head_group_norm(retention(rq, rk, rv)).reshape(B, S, RET_V_W).astype(x.dtype)
    y_ret = (o_ret * jax.nn.silu(rg)) @ w_ret_proj

    q = (rms_norm(cq, g_cq) @ w_uq).reshape(B, S, MLA_HEADS, MLA_NOPE + MLA_ROPE)
    q_nope, q_rope = q[..., :MLA_NOPE], q[..., MLA_NOPE:]
    q_rope = apply_rope(q_rope, cos_m[:, :, None], sin_m[:, :, None])
    kv = (rms_norm(ckv, g_ckv) @ w_ukv).reshape(B, S, MLA_HEADS, MLA_NOPE + MLA_DV)
    k_nope, v = kv[..., :MLA_NOPE], kv[..., MLA_NOPE:]
    k_rope = apply_rope(kr, cos_m, sin_m)
    o_mla = mla_attention(q_nope, q_rope, k_nope, k_rope, v)
    y_mla = (o_mla * jax.nn.silu(mg)) @ w_mla_proj

    g_a, g_b = jnp.split(jax.nn.sigmoid(bg), 2, axis=-1)
    merged = g_a * y_ret + g_b * y_mla
    out = merged @ w_out
    return x + gate[:, None, :] * out


def setup_inputs(seed: int = 0) -> dict:
    key = jax.random.key(seed)
    ks = jax.random.split(key, 16)
    f32 = jnp.float32

    def nrm(k, shape, fan_in, mult=1.0):
        return jax.random.normal(k, shape, f32) * (mult * fan_in ** -0.5)

    x = jax.random.normal(ks[0], (BATCH, SEQ, D_MODEL), f32)
    c = jax.random.normal(ks[1], (BATCH, D_MODEL), f32)
    positions = (jnp.arange(SEQ, dtype=jnp.int32)[None, :]
                 + jax.random.randint(ks[2], (BATCH, 1), 0, 1024, dtype=jnp.int32))
    w_mod = nrm(ks[3], (DEPTH, D_MODEL, 3 * D_MODEL), D_MODEL, 0.5)
    b_mod = 0.01 * jax.random.normal(ks[4], (DEPTH, 3 * D_MODEL), f32)
    g_norm = 1.0 + 0.02 * jax.random.normal(ks[5], (DEPTH, D_MODEL), f32)
    w_in = nrm(ks[6], (DEPTH, D_MODEL, D_IN), D_MODEL)
    g_cq = 1.0 + 0.02 * jax.random.normal(ks[7], (DEPTH, MLA_Q_RANK), f32)
    g_ckv = 1.0 + 0.02 * jax.random.normal(ks[8], (DEPTH, MLA_KV_RANK), f32)
    w_uq = nrm(ks[9], (DEPTH, MLA_Q_RANK, MLA_HEADS * (MLA_NOPE + MLA_ROPE)), MLA_Q_RANK)
    w_ukv = nrm(ks[10], (DEPTH, MLA_KV_RANK, MLA_HEADS * (MLA_NOPE + MLA_DV)), MLA_KV_RANK)
    w_ret_proj = nrm(ks[11], (DEPTH, RET_V_W, D_MODEL), RET_V_W)
    w_mla_proj = nrm(ks[12], (DEPTH, MLA_V_W, D_MODEL), MLA_V_W)
    w_out = nrm(ks[13], (DEPTH, D_MODEL, D_MODEL), D_MODEL)
    g_final = 1.0 + 0.02 * jax.random.normal(ks[14], (D_MODEL,), f32)
    return {"x": x, "c": c, "positions": positions, "w_mod": w_mod, "b_mod": b_mod,
            "g_norm": g_norm, "w_in": w_in, "g_cq": g_cq, "g_ckv": g_ckv, "w_uq": w_uq,
            "w_ukv": w_ukv, "w_ret_proj": w_ret_proj, "w_mla_proj": w_mla_proj,
            "w_out": w_out, "g_final": g_final}


def reference(x, c, positions, w_mod, b_mod, g_norm, w_in, g_cq, g_ckv, w_uq, w_ukv,
              w_ret_proj, w_mla_proj, w_out, g_final):
    c_act = jax.nn.silu(c)
    cos_r, sin_r = rope_tables(positions, RET_DK)
    cos_m, sin_m = rope_tables(positions, MLA_ROPE)
    for l in range(DEPTH):
        x = hybrid_layer(x, c_act, cos_r, sin_r, cos_m, sin_m,
                         w_mod[l], b_mod[l], g_norm[l], w_in[l], g_cq[l], g_ckv[l],
                         w_uq[l], w_ukv[l], w_ret_proj[l], w_mla_proj[l], w_out[l])
    return rms_norm(x, g_final)
```

```python
import numpy as np
from contextlib import ExitStack
import concourse.bass as bass
import concourse.mybir as mybir
from concourse.bass_utils import run_bass_kernel_spmd

F32 = mybir.dt.float32
BF16 = mybir.dt.bfloat16
I32 = mybir.dt.int32
AF = mybir.ActivationFunctionType
ALU = mybir.AluOpType

D = 2048
S = 4096
DEPTH = 4
D_IN = 13376
EPS = 1e-6
NTG = 8
TG = 512
O_RQ, O_RK, O_RV, O_RG, O_CQ, O_CKV, O_KR, O_MG, O_BG = 0, 1024, 2048, 4096, 6144, 6656, 7168, 7232, 9280
TWO_PI = 6.283185307179586
PI = 3.141592653589793


class Sem:
    def __init__(self, h, key):
        self.h = h
        self.key = key
        self.cnt = 0


class Buf:
    def __init__(self, t=None, sem=None):
        self.t = t
        self.sem = sem
        self.w = {}
        self.r = {}

    def __getitem__(self, k):
        return self.t[k]


class Eng:
    def __init__(self, name, h, sem):
        self.name = name
        self.h = h
        self.sem = sem
        self.waited = {}


class K:
    def __init__(self, nc, es):
        self.nc = nc
        self.es = es
        self.nsem = 0
        self.eng = {}
        for name, h in (("pe", nc.tensor), ("act", nc.scalar), ("dve", nc.vector),
                        ("pool", nc.gpsimd), ("sp", nc.sync)):
            self.eng[name] = Eng(name, h, self.newsem("e_" + name))
        self.tuid = 0
        self.free_sems = []
        self.free_sems_pool = []
        self.all_dsems = []
        self.uid = 0

    def newsem(self, name):
        h = self.es.enter_context(self.nc.semaphore(name))
        self.nsem += 1
        return Sem(h, name)

    def dsem(self, kind="sp"):
        fl = self.free_sems if kind == "sp" else self.free_sems_pool
        if fl:
            return fl.pop()
        self.uid += 1
        sm = self.newsem("d%d" % self.uid)
        self.all_dsems.append(sm)
        return sm

    def sb(self, es, name, shape, dt, dma=False):
        self.tuid += 1
        name = "%s_%d" % (name, self.tuid)
        t = es.enter_context(self.nc.sbuf_tensor(name, shape, dt))
        b = Buf(t, None)
        if dma:
            kind = "pool" if dma == "pool" else "sp"
            b.sem = self.dsem(kind)
            es.callback((self.free_sems if kind == "sp" else self.free_sems_pool).append, b.sem)
        return b

    def ps(self, es, name, shape, dt=F32):
        self.tuid += 1
        name = "%s_%d" % (name, self.tuid)
        t = es.enter_context(self.nc.psum_tensor(name, shape, dt))
        return Buf(t, None)

    def _wait(self, e, sem, val):
        if val <= 0:
            return
        if e.waited.get(sem.key, 0) >= val:
            return
        assert sem.cnt >= val, "wait on not-yet-emitted event %s %d>%d" % (sem.key, val, sem.cnt)
        e.h.wait_ge(sem.h, val)
        e.waited[sem.key] = val

    def op(self, en, fn, r=(), w=(), inc=True):
        e = self.eng[en]
        for b in r:
            for (sem, val) in b.w.values():
                self._wait(e, sem, val)
        for b in w:
            for (sem, val) in list(b.w.values()) + list(b.r.values()):
                if sem is e.sem:
                    continue
                self._wait(e, sem, val)
        ins = fn(e.h)
        if inc:
            e.sem.cnt += 1
            ins.then_inc(e.sem.h, 1)
            val = e.sem.cnt
        else:
            val = e.sem.cnt + 1
        ev = (e.sem, val)
        for b in r:
            b.r[e.sem.key] = ev
        for b in w:
            b.w = {e.sem.key: ev}
            b.r = {}
        return ins

    def dma(self, qn, out, in_, r=(), w=(), wplus=False, **kw):
        e = self.eng[qn]
        for b in r:
            for (sem, val) in b.w.values():
                self._wait(e, sem, val)
        for b in w:
            join = wplus and not b.r
            for (sem, val) in b.r.values():
                self._wait(e, sem, val)
            if not join:
                for (sem, val) in b.w.values():
                    self._wait(e, sem, val)
        sem = w[0].sem if w[0].t is not None else r[0].sem
        assert sem is not None
        ins = e.h.dma_start(out=out, in_=in_, **kw)
        sem.cnt += 16
        ins.then_inc(sem.h, 16)
        ev = (sem, sem.cnt)
        for b in r:
            b.r[sem.key] = ev
        for b in w:
            if wplus and not b.r:
                b.w[sem.key] = ev
            else:
                b.w = {sem.key: ev}
                b.r = {}
        return ins

    def barrier(self):
        sems = [e.sem for e in self.eng.values()] + self.all_dsems
        for e in self.eng.values():
            for sem in sems:
                if sem is e.sem:
                    continue
                self._wait(e, sem, sem.cnt)

    def wait_all(self, en, bufs):
        e = self.eng[en]
        for b in bufs:
            for (sem, val) in b.w.values():
                self._wait(e, sem, val)


class Ring:
    def __init__(self, bufs):
        self.bufs = bufs
        self.i = 0

    def next(self):
        b = self.bufs[self.i % len(self.bufs)]
        self.i += 1
        return b


def host_consts():
    p = np.arange(128)
    cv = np.zeros((128, 32), np.float64)
    cv[:, 0] = 10000.0 ** (-(2.0 * (p % 64)) / 128.0)
    cv[:, 1] = 10000.0 ** (-(2.0 * (p % 32)) / 64.0)
    cv[:, 2] = np.where(p < 64, -1.0, 1.0)
    cv[:, 3] = np.where((p % 64) < 32, -1.0, 1.0)
    rmask = np.zeros((128, 8 * 128), np.float64)
    for h in range(8):
        g = 1.0 - 2.0 ** (-5.0 - h)
        lg = np.log(g)
        cv[:, 4 + h] = np.exp((p + 1.0) * lg)
        cv[:, 12 + h] = np.exp((127.0 - p) * lg) * (128.0 ** -0.5)
        i = p[None, :].astype(np.float64)
        j = p[:, None].astype(np.float64)
        ci = (p[None, :] // 64)
        cj = (p[:, None] // 64)
        same = (ci == cj)
        earlier = (cj < ci)
        expo = np.where(same, np.abs(i - j), i - j) - (i + 1.0)
        m = np.where(same | earlier, np.exp(expo * lg), 0.0) * (128.0 ** -0.5)
        rmask[:, h * 128:(h + 1) * 128] = m
    ident = np.eye(128)
    return cv.astype(np.float32), rmask.astype(np.float32), ident.astype(np.float32)


GAMMA128 = [float((1.0 - 2.0 ** (-5.0 - h)) ** 128) for h in range(8)]


class Prog:
    def __init__(self, depth=DEPTH, debug=False, stop=None):
        self.depth = depth
        self.debug = debug
        self.stop = stop
        self.nc = bass.Bass("TRN2", target_bir_lowering=False)
        self.es = ExitStack()
        self.k = K(self.nc, self.es)

    def dram_in(self, name, shape, dt=F32):
        return self.nc.dram_tensor(name, list(shape), dt, kind="ExternalInput").ap()

    def dram_scratch(self, name, shape, dt, out=False):
        kind = "ExternalOutput" if (out or (self.debug and name in self.debug)) else "Internal"
        t = self.nc.dram_tensor(name, list(shape), dt, kind=kind).ap()
        return t

    def build(self):
        nc, k, es = self.nc, self.k, self.es
        L = self.depth
        self.xT = self.dram_in("xT", [D, S])
        self.c = self.dram_in("c", [1, D])
        self.pos = self.dram_in("pos", [1, S], I32)
        self.w_mod = self.dram_in("w_mod", [L, D, 3 * D])
        self.b_mod = self.dram_in("b_mod", [L, 3 * D])
        self.g_norm = self.dram_in("g_norm", [L, D])
        self.w_in = self.dram_in("w_in", [L, D, D_IN])
        self.g_cq = self.dram_in("g_cq", [L, 512])
        self.g_ckv = self.dram_in("g_ckv", [L, 512])
        self.w_uq = self.dram_in("w_uq", [L, 512, 3072])
        self.w_uqr = self.dram_in("w_uqr", [L, 512, 1024])
        self.w_ukv = self.dram_in("w_ukv", [L, 512, 4096])
        self.w_rp = self.dram_in("w_ret_proj", [L, D, D])
        self.w_mp = self.dram_in("w_mla_proj", [L, D, D])
        self.w_out = self.dram_in("w_out", [L, D, D])
        self.g_final = self.dram_in("g_final", [1, D])
        self.cvec_d = self.dram_in("cvec_d", [128, 32])
        self.rmask_d = self.dram_in("rmask_d", [128, 1024])
        self.ident_d = self.dram_in("ident_d", [128, 128])
        self.outT = self.nc.dram_tensor("outT", [D, S], F32, kind="ExternalOutput").ap()
        self.out_trk = Buf(None, None)
        ds = self.dram_scratch
        self.tabs = ds("tabs", [4, 128, S], F32)
        self.tabs_trk = Buf(None, None)
        self.xs = [ds("xs0", [D, S], F32), ds("xs1", [D, S], F32)]
        self.xs_trk = [Buf(None, None), Buf(None, None)]
        self.rqT = ds("rqT", [1024, S], BF16)
        self.rkT = ds("rkT", [1024, S], BF16)
        self.rv = ds("rv", [S, 2048], BF16)
        self.rgsT = ds("rgsT", [2048, S], BF16)
        self.cqT = ds("cqT", [512, S], F32)
        self.ckvT = ds("ckvT", [512, S], F32)
        self.kropeT = ds("kropeT", [128, S], BF16)
        self.mgsT = ds("mgsT", [2048, S], BF16)
        self.gabT = ds("gabT", [4096, S], BF16)
        self.B_trk = Buf(None, None)
        self.AretT = ds("AretT", [2048, S], BF16)
        self.R_trk = Buf(None, None)
        self.qT = ds("qT", [2048, S], BF16)
        self.qrT = ds("qrT", [1024, S], BF16)
        self.kT = ds("kT", [2048, S], BF16)
        self.vv = ds("vv", [S, 2048], BF16)
        self.C_trk = Buf(None, None)
        self.AmlaT = ds("AmlaT", [2048, S], BF16)
        self.D_trk = Buf(None, None)

        if self.stop == "A":
            self.hdbg = self.nc.dram_tensor("hdbg", [D, S], BF16, kind="ExternalOutput").ap()
        self.prologue()
        xin, xin_trk = self.xT, None
        for l in range(L):
            if self.stop == "P":
                break
            self.phase_AB(l, xin, xin_trk)
            if self.stop in ("A", "B"):
                break
            self.phase_R(l)
            if self.stop == "R":
                break
            self.phase_C(l)
            if self.stop == "C":
                break
            self.phase_D(l)
            if self.stop == "D":
                break
            self.phase_E(l, xin, xin_trk, self.xs[l % 2], self.xs_trk[l % 2])
            xin, xin_trk = self.xs[l % 2], self.xs_trk[l % 2]
        if self.stop is None:
            self.phase_F(xin, xin_trk)
        trks = [self.out_trk, self.tabs_trk, self.B_trk, self.R_trk, self.C_trk, self.D_trk] + self.xs_trk
        k.wait_all("sp", trks)
        self.es.close()
        return nc

    def wload(self, stg_ring, dst, dst_view, src_ap, nk, ncols, dup=False):
        k = self.k
        stg = stg_ring.next()
        k.dma("sp", stg[:, 0:nk, 0:ncols], src_ap, w=[stg])
        if dup:
            k.op("pool", lambda e: e.tensor_copy(dst[:, :, 0:ncols], stg[:, 0:nk, 0:ncols]), r=[stg], w=[dst])
            k.op("pool", lambda e: e.tensor_copy(dst[:, :, ncols:2 * ncols], stg[:, 0:nk, 0:ncols]), r=[stg], w=[dst])
        else:
            k.op("pool", lambda e: e.tensor_copy(dst_view, stg[:, 0:nk, 0:ncols]), r=[stg], w=[dst])

    def prologue(self):
        nc, k, es = self.nc, self.k, self.es
        L = self.depth
        sb = k.sb
        self.ident_b = sb(es, "ident_b", [128, 128], BF16)
        self.ident_f = sb(es, "ident_f", [128, 128], F32, dma=True)
        k.dma("sp", self.ident_f[:], self.ident_d, w=[self.ident_f])
        k.op("dve", lambda e: e.tensor_copy(self.ident_b[:], self.ident_f[:]), r=[self.ident_f], w=[self.ident_b])
        self.ones_f = sb(es, "ones_f", [128, 128], F32)
        k.op("dve", lambda e: e.memset(self.ones_f[:], 1.0), w=[self.ones_f])
        self.cvec = sb(es, "cvec", [128, 32], F32, dma=True)
        k.dma("sp", self.cvec[:], self.cvec_d, w=[self.cvec])
        self.rmask = sb(es, "rmask", [128, 1024], F32, dma=True)
        k.dma("sp", self.rmask[:], self.rmask_d, w=[self.rmask])
        self.modT = sb(es, "modT", [128, 48 * DEPTH], F32)
        self.gmod = sb(es, "gmod", [128, 16 * DEPTH], F32)
        self.gn = sb(es, "gn", [128, 16 * DEPTH], F32, dma=True)
        self.gcq = sb(es, "gcq", [128, 4 * DEPTH], F32, dma=True)
        self.gckv = sb(es, "gckv", [128, 4 * DEPTH], F32, dma=True)
        self.gfin = sb(es, "gfin", [128, 16], F32, dma=True)
        for l in range(L):
            k.dma("sp", self.gn[:, l * 16:(l + 1) * 16],
                  self.g_norm[l:l + 1, :].rearrange("o (kc p) -> p (o kc)", p=128),
                  w=[self.gn], wplus=True, allow_slow_non_contiguous=True)
            k.dma("sp", self.gcq[:, l * 4:(l + 1) * 4],
                  self.g_cq[l:l + 1, :].rearrange("o (kc p) -> p (o kc)", p=128),
                  w=[self.gcq], wplus=True, allow_slow_non_contiguous=True)
            k.dma("sp", self.gckv[:, l * 4:(l + 1) * 4],
                  self.g_ckv[l:l + 1, :].rearrange("o (kc p) -> p (o kc)", p=128),
                  w=[self.gckv], wplus=True, allow_slow_non_contiguous=True)
        k.dma("sp", self.gfin[:], self.g_final.rearrange("o (kc p) -> p (o kc)", p=128),
              w=[self.gfin], allow_slow_non_contiguous=True)

        with ExitStack() as ps:
            cT = sb(ps, "cT", [128, 16], F32, dma=True)
            k.dma("sp", cT[:], self.c.rearrange("o (kc p) -> p (o kc)", p=128), w=[cT],
                  allow_slow_non_contiguous=True)
            cact = sb(ps, "cact", [128, 16], F32)
            k.op("act", lambda e: e.activation(out=cact[:], in_=cT[:], func=AF.Silu), r=[cT], w=[cact])
            wm = Ring([sb(ps, "wm%d" % i, [128, 16, 512], F32, dma=True) for i in range(2)])
            brow = Ring([sb(ps, "brow%d" % i, [1, 6144], F32, dma=True) for i in range(2)])
            mrow = Ring([sb(ps, "mrow%d" % i, [1, 6144], F32) for i in range(2)])
            rps = Ring([k.ps(ps, "rps%d" % i, [1, 512]) for i in range(2)])
            mps = Ring([k.ps(ps, "mps%d" % i, [128, 48]) for i in range(2)])
            for l in range(L):
                br = brow.next()
                mr = mrow.next()
                k.dma("sp", br[:], self.b_mod[l:l + 1, :], w=[br])
                wview = self.w_mod[l].rearrange("(kc p) c -> p kc c", p=128)
                for cb in range(12):
                    ws = wm.next()
                    k.dma("sp", ws[:], wview[:, :, cb * 512:(cb + 1) * 512], w=[ws])
                    rp = rps.next()
                    for kc in range(16):
                        k.op("pe", lambda e, kc=kc: e.matmul(rp[:], cact[:, kc:kc + 1], ws[:, kc, :],
                                                             start=(kc == 0), stop=(kc == 15)),
                             r=[cact, ws], w=[rp], inc=(kc == 15))
                    k.op("dve", lambda e: e.tensor_tensor(out=mr[0:1, cb * 512:(cb + 1) * 512], in0=rp[:],
                                                          in1=br[0:1, cb * 512:(cb + 1) * 512], op=ALU.add),
                         r=[rp, br], w=[mr])
                mp = mps.next()
                for j in range(48):
                    k.op("pe", lambda e, j=j: e.matmul(mp[:, j:j + 1], mr[0:1, j * 128:(j + 1) * 128],
                                                       self.ones_f[0:1, 0:1], start=True, stop=True),
                         r=[mr, self.ones_f], w=[mp], inc=(j == 47))
                k.op("dve", lambda e: e.tensor_copy(self.modT[:, l * 48:(l + 1) * 48], mp[:]),
                     r=[mp], w=[self.modT])
                k.op("dve", lambda e: e.scalar_tensor_tensor(
                    out=self.gmod[:, l * 16:(l + 1) * 16], in0=self.modT[:, l * 48 + 16:l * 48 + 32], scalar=1.0,
                    in1=self.gn[:, l * 16:(l + 1) * 16], op0=ALU.add, op1=ALU.mult),
                     r=[self.modT, self.gn], w=[self.gmod])
        k.barrier()
        with ExitStack() as ps:
            posi = sb(ps, "posi", [128, S], I32, dma=True)
            k.dma("sp", posi[:], self.pos.partition_broadcast(128), w=[posi])
            posf = sb(ps, "posf", [128, S], F32)
            k.op("dve", lambda e: e.tensor_copy(posf[:], posi[:]), r=[posi], w=[posf])
            ang = sb(ps, "ang", [128, S], F32)
            kf = sb(ps, "kf", [128, S], F32)
            ki = sb(ps, "ki", [128, S], I32)
            rr = sb(ps, "rr", [128, S], F32)
            tb = Ring([sb(ps, "tb%d" % i, [128, S], F32, dma=True) for i in range(2)])
            for ti in range(4):
                invc = 0 if ti < 2 else 1
                is_cos = (ti % 2 == 0)
                sgnc = None if is_cos else (2 if ti < 2 else 3)
                k.op("dve", lambda e: e.tensor_scalar(out=ang[:], in0=posf[:], scalar1=self.cvec[:, invc:invc + 1],
                                                      scalar2=(PI / 2 if is_cos else 0.0), op0=ALU.mult, op1=ALU.add),
                     r=[posf, self.cvec], w=[ang])
                k.op("dve", lambda e: e.tensor_scalar(out=kf[:], in0=ang[:], scalar1=1.0 / TWO_PI, scalar2=None,
                                                      op0=ALU.mult), r=[ang], w=[kf])
                k.op("dve", lambda e: e.tensor_copy(ki[:], kf[:]), r=[kf], w=[ki])
                k.op("dve", lambda e: e.tensor_copy(kf[:], ki[:]), r=[ki], w=[kf])
                k.op("dve", lambda e: e.scalar_tensor_tensor(out=rr[:], in0=kf[:], scalar=-6.28125, in1=ang[:],
                                                             op0=ALU.mult, op1=ALU.add), r=[kf, ang], w=[rr])
                k.op("dve", lambda e: e.scalar_tensor_tensor(out=rr[:], in0=kf[:], scalar=-0.0019353071795864769,
                                                             in1=rr[:], op0=ALU.mult, op1=ALU.add), r=[kf, rr], w=[rr])
                k.op("dve", lambda e: e.tensor_scalar(out=kf[:], in0=rr[:], scalar1=PI, scalar2=-TWO_PI,
                                                      op0=ALU.is_gt, op1=ALU.mult), r=[rr], w=[kf])
                k.op("dve", lambda e: e.tensor_tensor(out=rr[:], in0=rr[:], in1=kf[:], op=ALU.add), r=[rr, kf], w=[rr])
                k.op("dve", lambda e: e.tensor_scalar(out=kf[:], in0=rr[:], scalar1=-PI, scalar2=TWO_PI,
                                                      op0=ALU.is_lt, op1=ALU.mult), r=[rr], w=[kf])
                k.op("dve", lambda e: e.tensor_tensor(out=rr[:], in0=rr[:], in1=kf[:], op=ALU.add), r=[rr, kf], w=[rr])
                k.op("dve", lambda e: e.tensor_scalar(out=rr[:], in0=rr[:], scalar1=-PI, scalar2=PI,
                                                      op0=ALU.max, op1=ALU.min), r=[rr], w=[rr])
                t = tb.next()
                if sgnc is None:
                    k.op("act", lambda e: e.activation(out=t[:], in_=rr[:], func=AF.Sin), r=[rr], w=[t])
                else:
                    k.op("act", lambda e: e.activation(out=t[:], in_=rr[:], func=AF.Sin), r=[rr], w=[t])
                    k.op("dve", lambda e: e.tensor_scalar(out=t[:], in0=t[:], scalar1=self.cvec[:, sgnc:sgnc + 1],
                                                          scalar2=None, op0=ALU.mult), r=[t, self.cvec], w=[t])
                k.dma("sp", self.tabs[ti], t[:], r=[t], w=[self.tabs_trk], wplus=True)

    def phase_AB(self, l, xin, xin_trk):
        nc, k = self.nc, self.k
        k.barrier()
        sb = k.sb
        xr = [xin_trk] if xin_trk is not None else []
        with ExitStack() as ps:
            hT = [sb(ps, "hT%d" % g, [128, 16, TG], BF16, dma=(self.stop == "A")) for g in range(NTG)]
            with ExitStack() as pa:
                X = Ring([sb(pa, "X%d" % i, [128, 16, 256], F32, dma=True) for i in range(2)])
                sq = Ring([sb(pa, "sq%d" % i, [128, 256], F32) for i in range(3)])
                rs = Ring([sb(pa, "rs%d" % i, [128, 256], F32) for i in range(2)])
                tt = Ring([sb(pa, "tt%d" % i, [128, 256], F32) for i in range(4)])
                sps = Ring([k.ps(pa, "sps%d" % i, [128, 256]) for i in range(2)])
                xv = xin.rearrange("(kc p) t -> p kc t", p=128)
                for g in range(16):
                    Xg = X.next()
                    k.dma("sp", Xg[:], xv[:, :, g * 256:(g + 1) * 256], r=xr, w=[Xg])
                    sp_ = sps.next()
                    for kc in range(16):
                        s = sq.next()
                        k.op("act", lambda e: e.activation(out=s[:], in_=Xg[:, kc, :], func=AF.Square), r=[Xg], w=[s])
                        k.op("pe", lambda e: e.matmul(sp_[:], self.ones_f[:], s[:], start=(kc == 0), stop=(kc == 15)),
                             r=[self.ones_f, s], w=[sp_], inc=True)
                    r_ = rs.next()
                    k.op("act", lambda e: e.activation(out=r_[:], in_=sp_[:], func=AF.Sqrt, bias=EPS, scale=1.0 / D),
                         r=[sp_], w=[r_])
                    k.op("dve", lambda e: e.reciprocal(out=r_[:], in_=r_[:]), r=[r_], w=[r_])
                    hb = hT[g // 2]
                    c0 = (g % 2) * 256
                    for kc in range(16):
                        t = tt.next()
                        k.op("dve", lambda e: e.tensor_tensor(out=t[:], in0=Xg[:, kc, :], in1=r_[:], op=ALU.mult),
                             r=[Xg, r_], w=[t])
                        k.op("act", lambda e: e.activation(
                            out=hb[:, kc, c0:c0 + 256], in_=t[:], func=AF.Identity,
                            bias=self.modT[:, l * 48 + kc:l * 48 + kc + 1],
                            scale=self.gmod[:, l * 16 + kc:l * 16 + kc + 1]),
                             r=[t, self.modT, self.gmod], w=[hb])
            if self.stop == "A":
                for g in range(NTG):
                    k.dma("sp", self.hdbg.rearrange("(kc p) t -> p kc t", p=128)[:, :, g * TG:(g + 1) * TG], hT[g][:],
                          r=[hT[g]], w=[self.B_trk], wplus=True)
                return
            k.barrier()
            W = Ring([sb(ps, "W%d" % i, [128, 16, 128], BF16) for i in range(3)])
            STG = Ring([sb(ps, "STG%d" % i, [128, 16, 128], F32, dma=True) for i in range(2)])
            PB = Ring([k.ps(ps, "PB%d" % i, [128, TG]) for i in range(6)])
            o32 = Ring([sb(ps, "o32_%d" % i, [128, TG], F32, dma=True) for i in range(2)])
            o16 = Ring([sb(ps, "o16_%d" % i, [128, TG], BF16, dma=True) for i in range(4)])
            ov = Ring([sb(ps, "ov_%d" % i, [128, 128], BF16, dma=True) for i in range(3)])
            swb = Ring([sb(ps, "swb%d" % i, [128, TG], F32) for i in range(2)])
            t1b = Ring([sb(ps, "t1b%d" % i, [128, TG], F32) for i in range(2)])
            Ct = Ring([sb(ps, "Ct%d" % i, [128, TG], F32, dma=True) for i in range(2)])
            St = Ring([sb(ps, "St%d" % i, [128, TG], F32, dma=True) for i in range(2)])
            wv_ = self.w_in[l].rearrange("(kc p) c -> p kc c", p=128)
            Bt = self.B_trk

            units = []
            for i in range(4):
                units.append(("raw32", O_CQ + i * 128, self.cqT[i * 128:(i + 1) * 128, :]))
            for i in range(4):
                units.append(("raw32", O_CKV + i * 128, self.ckvT[i * 128:(i + 1) * 128, :]))
            units.append(("rope_m", O_KR, self.kropeT[:, :]))
            for h in range(8):
                units.append(("rope_r", O_RQ + h * 128, self.rqT[h * 128:(h + 1) * 128, :]))
            for h in range(8):
                units.append(("rope_r", O_RK + h * 128, self.rkT[h * 128:(h + 1) * 128, :]))
            for i in range(16):
                units.append(("silu", O_RG + i * 128, self.rgsT[i * 128:(i + 1) * 128, :]))
            for i in range(16):
                units.append(("silu", O_MG + i * 128, self.mgsT[i * 128:(i + 1) * 128, :]))
            for i in range(32):
                units.append(("sigmoid", O_BG + i * 128, self.gabT[i * 128:(i + 1) * 128, :]))

            import os
            if os.environ.get("NUNITS"):
                units = units[:int(os.environ["NUNITS"])]

            def load_w(u):
                kind, c0, _ = units[u]
                ws = W.next()
                if kind == "rope_m":
                    self.wload(STG, ws, None, wv_[:, :, c0:c0 + 64], 16, 64, dup=True)
                else:
                    self.wload(STG, ws, ws[:], wv_[:, :, c0:c0 + 128], 16, 128)
                return ws

            PF = 2
            wq = [load_w(u) for u in range(min(PF, len(units)))]
            for u, (kind, c0, dst) in enumerate(units):
                ws = wq.pop(0)
                if u + PF < len(units):
                    wq.append(load_w(u + PF))
                for g in range(NTG):
                    pb = PB.next()
                    for kc in range(16):
                        k.op("pe", lambda e: e.matmul(pb[:], ws[:, kc, :], hT[g][:, kc, :],
                                                      start=(kc == 0), stop=(kc == 15)),
                             r=[ws, hT[g]], w=[pb], inc=(kc == 15))
                    cs = slice(g * TG, (g + 1) * TG)
                    if kind == "raw32":
                        o = o32.next()
                        k.op("act", lambda e: e.copy(out=o[:], in_=pb[:]), r=[pb], w=[o])
                    elif kind in ("silu", "sigmoid"):
                        o = o16.next()
                        fn = AF.Silu if kind == "silu" else AF.Sigmoid
                        k.op("act", lambda e: e.activation(out=o[:], in_=pb[:], func=fn), r=[pb], w=[o])
                    else:
                        ti = 0 if kind == "rope_r" else 2
                        c_, s_ = Ct.next(), St.next()
                        k.dma("sp", c_[:], self.tabs[ti][:, cs], r=[self.tabs_trk], w=[c_])
                        k.dma("sp", s_[:], self.tabs[ti + 1][:, cs], r=[self.tabs_trk], w=[s_])
                        sw = swb.next()
                        hw = 64 if kind == "rope_r" else 32
                        for q0 in range(0, 128, 2 * hw):
                            k.op("act", lambda e: e.copy(out=sw[q0:q0 + hw, :], in_=pb[q0 + hw:q0 + 2 * hw, :]),
                                 r=[pb], w=[sw])
                            k.op("act", lambda e: e.copy(out=sw[q0 + hw:q0 + 2 * hw, :], in_=pb[q0:q0 + hw, :]),
                                 r=[pb], w=[sw])
                        t1 = t1b.next()
                        k.op("dve", lambda e: e.tensor_tensor(out=t1[:], in0=pb[:], in1=c_[:], op=ALU.mult),
                             r=[pb, c_, sw], w=[t1])
                        k.op("dve", lambda e: e.tensor_tensor(out=sw[:], in0=sw[:], in1=s_[:], op=ALU.mult),
                             r=[sw, s_], w=[sw])
                        o = o16.next()
                        k.op("dve", lambda e: e.tensor_tensor(out=o[:], in0=t1[:], in1=sw[:], op=ALU.add),
                             r=[t1, sw], w=[o])
                    k.dma("sp", dst[:, cs], o[:], r=[o], w=[Bt], wplus=True)
            def load_wv(cb):
                ws = W.next()
                self.wload(STG, ws, ws[:], wv_[:, :, O_RV + cb * 128:O_RV + (cb + 1) * 128], 16, 128)
                return ws
            NCB = 0 if os.environ.get("NUNITS") else 16
            wq = [load_wv(cb) for cb in range(min(2, NCB))]
            for cb in range(NCB):
                ws = wq.pop(0)
                if cb + 2 < NCB:
                    wq.append(load_wv(cb + 2))
                for tt_ in range(32):
                    pb = PB.next()
                    g, j = tt_ // 4, tt_ % 4
                    for kc in range(16):
                        k.op("pe", lambda e: e.matmul(pb[:, 0:128], hT[g][:, kc, j * 128:(j + 1) * 128], ws[:, kc, :],
                                                      start=(kc == 0), stop=(kc == 15)),
                             r=[ws, hT[g]], w=[pb], inc=(kc == 15))
                    o = ov.next()
                    if tt_ % 2 == 0:
                        k.op("act", lambda e: e.copy(out=o[:], in_=pb[:, 0:128]), r=[pb], w=[o])
                    else:
                        k.op("dve", lambda e: e.tensor_copy(o[:], pb[:, 0:128]), r=[pb], w=[o])
                    k.dma("sp", self.rv[tt_ * 128:(tt_ + 1) * 128, cb * 128:(cb + 1) * 128], o[:],
                          r=[o], w=[Bt], wplus=True)

    def phase_R(self, l):
        nc, k = self.nc, self.k
        k.barrier()
        sb = k.sb
        Bt = self.B_trk
        with ExitStack() as ps:
            qh = Ring([sb(ps, "qh%d" % i, [128, S], BF16, dma=True) for i in range(2)])
            kh = Ring([sb(ps, "kh%d" % i, [128, S], BF16, dma=True) for i in range(2)])
            vh = Ring([sb(ps, "vh%d" % i, [128, 32, 256], BF16, dma=True) for i in range(2)])
            gh = Ring([sb(ps, "gh%d" % i, [128, 2, S], BF16, dma=True) for i in range(2)])
            Ah = Ring([sb(ps, "Ah%d" % i, [128, 2, S], BF16, dma=True) for i in range(2)])
            Rf = sb(ps, "Rf", [128, 256], F32)
            Rb = Ring([sb(ps, "Rb%d" % i, [128, 256], BF16) for i in range(2)])
            kz = Ring([sb(ps, "kz%d" % i, [128, 128], BF16) for i in range(3)])
            PT = Ring([sb(ps, "PTr%d" % i, [128, 128], BF16) for i in range(3)])
            of = Ring([sb(ps, "of%d" % i, [128, 256], F32) for i in range(3)])
            st = Ring([sb(ps, "st%d" % i, [128, 8], F32) for i in range(3)])
            sd = Ring([sb(ps, "sd%d" % i, [128, 1], F32) for i in range(3)])
            on = Ring([sb(ps, "on%d" % i, [128, 256], BF16) for i in range(3)])
            kT_ps = Ring([k.ps(ps, "kTps0", [128, 128], BF16)])
            tr_ps = Ring([k.ps(ps, "trps0", [128, 2, 128], BF16)])
            s_ps = Ring([k.ps(ps, "sps_r%d" % i, [128, 128]) for i in range(2)])
            o_ps = Ring([k.ps(ps, "ops%d" % i, [128, 256]) for i in range(2)])
            dR_ps = Ring([k.ps(ps, "dRps%d" % i, [128, 256]) for i in range(2)])
            rvv = self.rv.rearrange("(n p) c -> p n c", p=128)

            def load_head(h):
                q_, k_, v_, g_ = qh.next(), kh.next(), vh.next(), gh.next()
                k.dma("sp", q_[:], self.rqT[h * 128:(h + 1) * 128, :], r=[Bt], w=[q_])
                k.dma("sp", k_[:], self.rkT[h * 128:(h + 1) * 128, :], r=[Bt], w=[k_])
                for i in range(4):
                    k.dma("sp", v_[:, i * 8:(i + 1) * 8, :], rvv[:, i * 8:(i + 1) * 8, h * 256:(h + 1) * 256],
                          r=[Bt], w=[v_], wplus=(i > 0))
                for e2 in range(2):
                    k.dma("sp", g_[:, e2, :], self.rgsT[h * 256 + e2 * 128:h * 256 + (e2 + 1) * 128, :],
                          r=[Bt], w=[g_], wplus=(e2 > 0))
                return q_, k_, v_, g_

            nxt = load_head(0)
            for h in range(8):
                q_, k_, v_, g_ = nxt
                if h + 1 < 8:
                    nxt = load_head(h + 1)
                A_ = Ah.next()
                zc = self.cvec[:, 12 + h:13 + h]
                xc = self.cvec[:, 4 + h:5 + h]
                mk = self.rmask[:, h * 128:(h + 1) * 128]
                stA = {}
                rb_cur = [None]

                def stageA(n):
                    cs = slice(n * 128, (n + 1) * 128)
                    kt = kT_ps.next()
                    k.op("pe", lambda e: e.transpose(kt[:], k_[:, cs], self.ident_b[:]), r=[k_, self.ident_b], w=[kt])
                    kz_ = kz.next()
                    k.op("act", lambda e: e.activation(out=kz_[:], in_=kt[:], func=AF.Copy, scale=zc),
                         r=[kt, self.cvec], w=[kz_])
                    sp_ = s_ps.next()
                    k.op("pe", lambda e: e.matmul(sp_[:], k_[:, cs], q_[:, cs], start=True, stop=True),
                         r=[k_, q_], w=[sp_])
                    pt = PT.next()
                    k.op("dve", lambda e: e.tensor_tensor(out=pt[:], in0=sp_[:], in1=mk, op=ALU.mult),
                         r=[sp_, self.rmask], w=[pt])
                    stA[n] = (kz_, pt)

                def stageB(n):
                    cs = slice(n * 128, (n + 1) * 128)
                    kz_, pt = stA.pop(n)
                    op_ = o_ps.next()
                    k.op("pe", lambda e: e.matmul(op_[:], pt[:], v_[:, n, :], start=True, stop=(n == 0)),
                         r=[pt, v_], w=[op_], inc=(n == 0))
                    if n > 0:
                        rb = rb_cur[0]
                        k.op("pe", lambda e: e.matmul(op_[:], q_[:, cs], rb[:], start=False, stop=True),
                             r=[q_, rb], w=[op_])
                    dp = dR_ps.next()
                    k.op("pe", lambda e: e.matmul(dp[:], kz_[:], v_[:, n, :], start=True, stop=True),
                         r=[kz_, v_], w=[dp])
                    if n == 0:
                        k.op("dve", lambda e: e.tensor_copy(Rf[:], dp[:]), r=[dp], w=[Rf])
                    else:
                        k.op("dve", lambda e: e.scalar_tensor_tensor(out=Rf[:], in0=Rf[:], scalar=GAMMA128[h], in1=dp[:],
                                                                     op0=ALU.mult, op1=ALU.add), r=[Rf, dp], w=[Rf])
                    if n < 31:
                        rbn = Rb.next()
                        k.op("act", lambda e: e.copy(out=rbn[:], in_=Rf[:]), r=[Rf], w=[rbn])
                        rb_cur[0] = rbn
                    o_ = of.next()
                    k.op("act", lambda e: e.activation(out=o_[:], in_=op_[:], func=AF.Copy, scale=xc),
                         r=[op_, self.cvec], w=[o_])
                    s_ = st.next()
                    k.op("dve", lambda e: e.bn_stats(out=s_[:, 0:6], in_=o_[:]), r=[o_], w=[s_])
                    k.op("dve", lambda e: e.bn_aggr(out=s_[:, 6:8], in_=s_[:, 0:6]), r=[s_], w=[s_])
                    d_ = sd.next()
                    k.op("act", lambda e: e.activation(out=d_[:], in_=s_[:, 7:8], func=AF.Sqrt, bias=EPS, scale=1.0),
                         r=[s_], w=[d_])
                    k.op("dve", lambda e: e.reciprocal(out=d_[:], in_=d_[:]), r=[d_], w=[d_])
                    n_ = on.next()
                    k.op("dve", lambda e: e.tensor_scalar(out=n_[:], in0=o_[:], scalar1=s_[:, 6:7], scalar2=d_[:, 0:1],
                                                          op0=ALU.subtract, op1=ALU.mult), r=[o_, s_, d_], w=[n_])
                    tp = tr_ps.next()
                    for e2 in range(2):
                        k.op("pe", lambda e: e.transpose(tp[:, e2, :], n_[:, e2 * 128:(e2 + 1) * 128], self.ident_b[:]),
                             r=[n_, self.ident_b], w=[tp], inc=(e2 == 1))
                    k.op("dve", lambda e: e.tensor_tensor(out=A_[:, :, cs], in0=tp[:], in1=g_[:, :, cs], op=ALU.mult),
                         r=[tp, g_], w=[A_])

                stageA(0)
                for n in range(32):
                    if n + 1 < 32:
                        stageA(n + 1)
                    stageB(n)
                for e2 in range(2):
                    k.dma("sp", self.AretT[h * 256 + e2 * 128:h * 256 + (e2 + 1) * 128, :], A_[:, e2, :],
                          r=[A_], w=[self.R_trk], wplus=True)

    def phase_C(self, l):
        nc, k = self.nc, self.k
        k.barrier()
        sb = k.sb
        Bt, Ct_ = self.B_trk, self.C_trk
        with ExitStack() as ps:
            cqg = sb(ps, "cqg", [128, 4, S], BF16)
            ckvg = sb(ps, "ckvg", [128, 4, S], BF16)
            rq_rep = sb(ps, "rq_rep", [128, S], F32)
            rkv_rep = sb(ps, "rkv_rep", [128, S], F32)
            rcol = sb(ps, "rcol", [128, 32], F32)
            Cm = sb(ps, "Cm", [128, S], F32, dma=True)
            Sm = sb(ps, "Sm", [128, S], F32, dma=True)
            k.dma("sp", Cm[:], self.tabs[2], r=[self.tabs_trk], w=[Cm])
            k.dma("sp", Sm[:], self.tabs[3], r=[self.tabs_trk], w=[Sm])
            with ExitStack() as p0:
                HS = S // 2
                raw = sb(p0, "rawc", [128, 4, HS], F32, dma=True)
                sqf = Ring([sb(p0, "sqf%d" % i, [128, 4, TG], F32) for i in range(2)])
                rsq = Ring([sb(p0, "rsq%d" % i, [128, TG], F32) for i in range(2)])
                ssq_ps = Ring([k.ps(p0, "ssqc%d" % i, [128, TG]) for i in range(2)])
                col_ps = k.ps(p0, "colps", [128, 32])
                for which in range(2):
                    src = self.cqT if which == 0 else self.ckvT
                    gvec = self.gcq if which == 0 else self.gckv
                    dst = cqg if which == 0 else ckvg
                    rep = rq_rep if which == 0 else rkv_rep
                    for half in range(2):
                        for kc in range(4):
                            k.dma("sp", raw[:, kc, :], src[kc * 128:(kc + 1) * 128, half * HS:(half + 1) * HS],
                                  r=[Bt], w=[raw], wplus=(kc > 0))
                        for gl in range(NTG // 2):
                            g = half * (NTG // 2) + gl
                            cs = slice(g * TG, (g + 1) * TG)
                            ls = slice(gl * TG, (gl + 1) * TG)
                            sp_ = ssq_ps.next()
                            s_ = sqf.next()
                            for kc in range(4):
                                k.op("act", lambda e: e.activation(out=s_[:, kc, :], in_=raw[:, kc, ls], func=AF.Square),
                                     r=[raw], w=[s_])
                            for kc in range(4):
                                k.op("pe", lambda e: e.matmul(sp_[:], self.ones_f[:], s_[:, kc, :], start=(kc == 0), stop=(kc == 3)),
                                     r=[self.ones_f, s_], w=[sp_], inc=(kc == 3))
                            r_ = rsq.next()
                            k.op("act", lambda e: e.activation(out=r_[:], in_=sp_[:], func=AF.Sqrt, bias=EPS, scale=1.0 / 512),
                                 r=[sp_], w=[r_])
                            k.op("dve", lambda e: e.reciprocal(out=rep[:, cs], in_=r_[:]), r=[r_], w=[rep])
                        for kc in range(4):
                            k.op("dve", lambda e: e.tensor_scalar(out=dst[:, kc, half * HS:(half + 1) * HS], in0=raw[:, kc, :],
                                                                  scalar1=gvec[:, l * 4 + kc:l * 4 + kc + 1], scalar2=None,
                                                                  op0=ALU.mult), r=[raw, gvec], w=[dst])
                for tt_ in range(32):
                    k.op("pe", lambda e: e.matmul(col_ps[:, tt_:tt_ + 1], rkv_rep[0:1, tt_ * 128:(tt_ + 1) * 128],
                                                  self.ones_f[0:1, 0:1], start=True, stop=True),
                         r=[rkv_rep, self.ones_f], w=[col_ps], inc=(tt_ == 31))
                k.op("dve", lambda e: e.tensor_copy(rcol[:], col_ps[:]), r=[col_ps], w=[rcol])
            import os
            CS = int(os.environ.get("CSTOP", "0"))
            if CS == 1:
                return
            k.barrier()
            Wc = Ring([sb(ps, "Wc%d" % i, [128, 4, 128], BF16) for i in range(4)])
            Wv = Ring([sb(ps, "Wvc%d" % i, [128, 4, 512], BF16) for i in range(2)])
            STGc = Ring([sb(ps, "STGc%d" % i, [128, 4, 128], F32, dma=True) for i in range(3)])
            PB = Ring([k.ps(ps, "PBc%d" % i, [128, TG]) for i in range(5)])
            o16 = Ring([sb(ps, "oc16_%d" % i, [128, TG], BF16, dma=True) for i in range(4)])
            swb = Ring([sb(ps, "swc%d" % i, [128, TG], F32) for i in range(2)])
            t1b = Ring([sb(ps, "t1c%d" % i, [128, TG], F32) for i in range(2)])
            t2b = Ring([sb(ps, "t2c%d" % i, [128, TG], F32) for i in range(2)])
            uqv = self.w_uq[l].rearrange("(kc p) c -> p kc c", p=128)
            uqrv = self.w_uqr[l].rearrange("(kc p) c -> p kc c", p=128)
            ukvv = self.w_ukv[l].rearrange("(kc p) c -> p kc c", p=128)
            units = []
            for h in range(16):
                units.append(("q", uqv[:, :, h * 192:h * 192 + 128], self.qT[h * 128:(h + 1) * 128, :]))
            for p_ in range(8):
                units.append(("qr", uqrv[:, :, p_ * 128:(p_ + 1) * 128], self.qrT[p_ * 128:(p_ + 1) * 128, :]))
            for h in range(16):
                units.append(("k", ukvv[:, :, h * 256:h * 256 + 128], self.kT[h * 128:(h + 1) * 128, :]))

            def load_w(u):
                ws = Wc.next()
                self.wload(STGc, ws, ws[:], units[u][1], 4, 128)
                return ws
            if CS == 2:
                units = units[:4]
            if CS == 3:
                units = units[16:20]
            if CS == 4:
                units = units[24:28]
            if CS == 5:
                units = units[:3]
            PF = 3
            wq = [load_w(u) for u in range(PF)]
            for u, (kind, _, dst) in enumerate(units):
                ws = wq.pop(0)
                if u + PF < len(units):
                    wq.append(load_w(u + PF))
                src = ckvg if kind == "k" else cqg
                rep = rkv_rep if kind == "k" else rq_rep
                for g in range(NTG):
                    cs = slice(g * TG, (g + 1) * TG)
                    pb = PB.next()
                    for kc in range(4):
                        k.op("pe", lambda e: e.matmul(pb[:], ws[:, kc, :], src[:, kc, cs], start=(kc == 0), stop=(kc == 3)),
                             r=[ws, src], w=[pb], inc=(kc == 3))
                    o = o16.next()
                    if kind in ("q", "k"):
                        k.op("dve", lambda e: e.tensor_tensor(out=o[:], in0=pb[:], in1=rep[:, cs], op=ALU.mult),
                             r=[pb, rep], w=[o])
                    else:
                        sw = swb.next()
                        for q0 in range(0, 128, 64):
                            k.op("act", lambda e: e.copy(out=sw[q0:q0 + 32, :], in_=pb[q0 + 32:q0 + 64, :]), r=[pb], w=[sw])
                            k.op("act", lambda e: e.copy(out=sw[q0 + 32:q0 + 64, :], in_=pb[q0:q0 + 32, :]), r=[pb], w=[sw])
                        t1 = t1b.next()
                        k.op("dve", lambda e: e.tensor_tensor(out=t1[:], in0=pb[:], in1=Cm[:, cs], op=ALU.mult),
                             r=[pb, Cm, sw], w=[t1])
                        k.op("dve", lambda e: e.tensor_tensor(out=sw[:], in0=sw[:], in1=Sm[:, cs], op=ALU.mult),
                             r=[sw, Sm], w=[sw])
                        t2 = t2b.next()
                        k.op("dve", lambda e: e.tensor_tensor(out=t2[:], in0=t1[:], in1=sw[:], op=ALU.add),
                             r=[t1, sw], w=[t2])
                        k.op("act", lambda e: e.copy(out=t1[:], in_=rep[:, cs]), r=[rep], w=[t1])
                        k.op("dve", lambda e: e.tensor_tensor(out=o[:], in0=t2[:], in1=t1[:], op=ALU.mult),
                             r=[t2, t1], w=[o])
                    k.dma("sp", dst[:, cs], o[:], r=[o], w=[Ct_], wplus=True)
            for hg in range(0 if CS in (2, 3, 4) else 4):
                ws = Wv.next()
                for j in range(4):
                    h = hg * 4 + j
                    self.wload(STGc, ws, ws[:, :, j * 128:(j + 1) * 128], ukvv[:, :, h * 256 + 128:h * 256 + 256], 4, 128)
                for tt_ in range(32):
                    pb = PB.next()
                    for kc in range(4):
                        k.op("pe", lambda e: e.matmul(pb[:], ckvg[:, kc, tt_ * 128:(tt_ + 1) * 128], ws[:, kc, :],
                                                      start=(kc == 0), stop=(kc == 3)),
                             r=[ckvg, ws], w=[pb], inc=(kc == 3))
                    o = o16.next()
                    k.op("act", lambda e: e.activation(out=o[:], in_=pb[:], func=AF.Copy, scale=rcol[:, tt_:tt_ + 1]),
                         r=[pb, rcol], w=[o])
                    k.dma("sp", self.vv[tt_ * 128:(tt_ + 1) * 128, hg * 512:(hg + 1) * 512], o[:],
                          r=[o], w=[Ct_], wplus=True)

    def phase_D(self, l):
        nc, k = self.nc, self.k
        k.barrier()
        sb = k.sb
        Bt, Ct_ = self.B_trk, self.C_trk
        SCALE = 192.0 ** -0.5
        with ExitStack() as ps:
            Qh = Ring([sb(ps, "Qh%d" % i, [128, S], BF16, dma=True) for i in range(2)])
            QR = Ring([sb(ps, "QR%d" % i, [128, S], BF16, dma=True) for i in range(2)])
            Kh = Ring([sb(ps, "Kh%d" % i, [128, S], BF16, dma=True) for i in range(2)])
            Vh = Ring([sb(ps, "Vh%d" % i, [128, 32, 128], BF16, dma=True) for i in range(2)])
            Gh = Ring([sb(ps, "Gh%d" % i, [128, S], BF16, dma=True) for i in range(2)])
            Oh = Ring([sb(ps, "Oh%d" % i, [128, S], BF16, dma=True) for i in range(2)])
            KR = sb(ps, "KR", [128, S], BF16, dma=True)
            k.dma("sp", KR[:], self.kropeT[:, :], r=[Bt], w=[KR])
            PT = Ring([sb(ps, "PTd%d" % i, [128, TG], BF16) for i in range(4)])
            lacc = Ring([sb(ps, "lacc%d" % i, [128, TG], F32) for i in range(2)])
            rl = Ring([sb(ps, "rl%d" % i, [128, TG], F32) for i in range(2)])
            tmp = Ring([sb(ps, "tmpd%d" % i, [128, TG], F32) for i in range(2)])
            S_ps = Ring([k.ps(ps, "Sps%d" % i, [128, TG]) for i in range(3)])
            O_ps = Ring([k.ps(ps, "Ops%d" % i, [128, TG]) for i in range(2)])
            L_ps = Ring([k.ps(ps, "Lps%d" % i, [128, TG]) for i in range(2)])
            vvv = self.vv.rearrange("(n p) c -> p n c", p=128)

            def load_head(h, qr_prev):
                q_, k_, v_, g_ = Qh.next(), Kh.next(), Vh.next(), Gh.next()
                k.dma("sp", q_[:], self.qT[h * 128:(h + 1) * 128, :], r=[Ct_], w=[q_])
                k.dma("sp", k_[:], self.kT[h * 128:(h + 1) * 128, :], r=[Ct_], w=[k_])
                for i in range(4):
                    k.dma("sp", v_[:, i * 8:(i + 1) * 8, :], vvv[:, i * 8:(i + 1) * 8, h * 128:(h + 1) * 128],
                          r=[Ct_], w=[v_], wplus=(i > 0))
                k.dma("sp", g_[:], self.mgsT[h * 128:(h + 1) * 128, :], r=[Bt], w=[g_])
                if h % 2 == 0:
                    qr_ = QR.next()
                    k.dma("sp", qr_[:], self.qrT[(h // 2) * 128:(h // 2 + 1) * 128, :], r=[Ct_], w=[qr_])
                else:
                    qr_ = qr_prev
                return q_, k_, v_, g_, qr_

            nxt = load_head(0, None)
            for h in range(16):
                q_, k_, v_, g_, qr_ = nxt
                if h + 1 < 16:
                    nxt = load_head(h + 1, qr_)
                O_ = Oh.next()
                po = 64 * (h % 2)
                tiles = []
                for g in range(NTG):
                    nt = 4 * g + 4
                    for kt in range(nt):
                        t = kt - 4 * g
                        off = 128 * t if t > 0 else 0
                        tiles.append((g, kt, off, t >= 0, kt == 0, kt == nt - 1))
                state = {}
                gstate = {}

                def emit_S(i):
                    g, kt, off, diag, first, last = tiles[i]
                    q0 = g * TG + off
                    q1 = (g + 1) * TG
                    ks = slice(kt * 128, (kt + 1) * 128)
                    sp_ = S_ps.next()
                    k.op("pe", lambda e: e.matmul(sp_[:, off:TG], k_[:, ks], q_[:, q0:q1], start=True, stop=False),
                         r=[k_, q_], w=[sp_], inc=False)
                    k.op("pe", lambda e: e.matmul(sp_[:, off:TG], KR[po:po + 64, ks], qr_[po:po + 64, q0:q1],
                                                  start=False, stop=True), r=[KR, qr_], w=[sp_])
                    pt = PT.next()
                    k.op("act", lambda e: e.activation(out=pt[:, off:TG], in_=sp_[:, off:TG], func=AF.Exp, scale=SCALE),
                         r=[sp_], w=[pt])
                    if diag:
                        k.op("dve", lambda e: e.memset(pt[64:128, off:off + 64], 0.0), w=[pt])
                    if first:
                        gstate[g] = (lacc.next(), O_ps.next())
                    la, op_ = gstate[g]
                    if first:
                        k.op("dve", lambda e: e.tensor_copy(la[:], pt[:]), r=[pt], w=[la])
                    else:
                        k.op("dve", lambda e: e.tensor_tensor(out=la[:, off:TG], in0=la[:, off:TG], in1=pt[:, off:TG],
                                                              op=ALU.add), r=[la, pt], w=[la])
                    state[i] = pt

                def emit_PV(i):
                    g, kt, off, diag, first, last = tiles[i]
                    pt = state.pop(i)
                    la, op_ = gstate[g]
                    cs = slice(g * TG, (g + 1) * TG)
                    k.op("pe", lambda e: e.matmul(op_[:, off:TG], v_[:, kt, :], pt[:, off:TG], start=first, stop=last),
                         r=[v_, pt], w=[op_], inc=True)
                    if last:
                        lp = L_ps.next()
                        k.op("pe", lambda e: e.matmul(lp[:], self.ones_f[:], la[:], start=True, stop=True),
                             r=[self.ones_f, la], w=[lp])
                        r_ = rl.next()
                        k.op("dve", lambda e: e.reciprocal(out=r_[:], in_=lp[:]), r=[lp], w=[r_])
                        t_ = tmp.next()
                        k.op("dve", lambda e: e.tensor_tensor(out=t_[:], in0=op_[:], in1=r_[:], op=ALU.mult),
                             r=[op_, r_], w=[t_])
                        k.op("dve", lambda e: e.tensor_tensor(out=O_[:, cs], in0=t_[:], in1=g_[:, cs], op=ALU.mult),
                             r=[t_, g_], w=[O_])
                        del gstate[g]

                LA = 2
                n = len(tiles)
                for i in range(n + LA):
                    if i < n:
                        emit_S(i)
                    if i >= LA:
                        emit_PV(i - LA)
                k.dma("sp", self.AmlaT[h * 128:(h + 1) * 128, :], O_[:], r=[O_], w=[self.D_trk], wplus=True)

    def phase_E(self, l, xin, xin_trk, xout, xout_trk):
        nc, k = self.nc, self.k
        k.barrier()
        sb = k.sb
        Bt = self.B_trk
        xr = [xin_trk] if xin_trk is not None else []
        QT = 1024
        with ExitStack() as ps:
            Ar = sb(ps, "Ar", [128, 16, QT], BF16, dma=True)
            Am = sb(ps, "Am", [128, 16, QT], BF16, dma=True)
            Mg = sb(ps, "Mg", [128, 16, QT], BF16)
            Wr = Ring([sb(ps, "Wr%d" % i, [128, 16, 128], BF16) for i in range(3)])
            Wm = Ring([sb(ps, "Wm%d" % i, [128, 16, 128], BF16) for i in range(3)])
            Wo = Ring([sb(ps, "Wo%d" % i, [128, 16, 128], BF16) for i in range(3)])
            STGe = Ring([sb(ps, "STGe%d" % i, [128, 16, 128], F32, dma=True) for i in range(2)])
            ga = Ring([sb(ps, "ga%d" % i, [128, QT], BF16, dma=True) for i in range(2)])
            gb = Ring([sb(ps, "gb%d" % i, [128, QT], BF16, dma=True) for i in range(2)])
            xo = Ring([sb(ps, "xo%d" % i, [128, QT], F32, dma=True) for i in range(2)])
            m1 = Ring([sb(ps, "m1_%d" % i, [128, TG], F32) for i in range(2)])
            m2 = Ring([sb(ps, "m2_%d" % i, [128, TG], F32) for i in range(2)])
            xn = Ring([sb(ps, "xn%d" % i, [128, TG], F32, dma=True) for i in range(3)])
            Yr = Ring([k.ps(ps, "Yr%d" % i, [128, TG]) for i in range(2)])
            Ym = Ring([k.ps(ps, "Ym%d" % i, [128, TG]) for i in range(2)])
            Yo = Ring([k.ps(ps, "Yo%d" % i, [128, TG]) for i in range(3)])
            arv = self.AretT.rearrange("(kc p) t -> p kc t", p=128)
            amv = self.AmlaT.rearrange("(kc p) t -> p kc t", p=128)
            wrv = self.w_rp[l].rearrange("(kc p) c -> p kc c", p=128)
            wmv = self.w_mp[l].rearrange("(kc p) c -> p kc c", p=128)
            wov = self.w_out[l].rearrange("(kc p) c -> p kc c", p=128)

            def load_A(qt):
                ts = slice(qt * QT, (qt + 1) * QT)
                for i in range(4):
                    k.dma("sp", Ar[:, i * 4:(i + 1) * 4, :], arv[:, i * 4:(i + 1) * 4, ts], r=[self.R_trk], w=[Ar],
                          wplus=(i > 0))
                for i in range(4):
                    k.dma("sp", Am[:, i * 4:(i + 1) * 4, :], amv[:, i * 4:(i + 1) * 4, ts], r=[self.D_trk], w=[Am],
                          wplus=(i > 0))

            def load_y(qt, fb):
                ts = slice(qt * QT, (qt + 1) * QT)
                a, b, c, d = Wr.next(), Wm.next(), ga.next(), gb.next()
                self.wload(STGe, a, a[:], wrv[:, :, fb * 128:(fb + 1) * 128], 16, 128)
                self.wload(STGe, b, b[:], wmv[:, :, fb * 128:(fb + 1) * 128], 16, 128)
                k.dma("sp", c[:], self.gabT[fb * 128:(fb + 1) * 128, ts], r=[Bt], w=[c])
                k.dma("sp", d[:], self.gabT[2048 + fb * 128:2048 + (fb + 1) * 128, ts], r=[Bt], w=[d])
                return a, b, c, d

            def load_o(qt, fb):
                ts = slice(qt * QT, (qt + 1) * QT)
                a, b = Wo.next(), xo.next()
                self.wload(STGe, a, a[:], wov[:, :, fb * 128:(fb + 1) * 128], 16, 128)
                k.dma("sp", b[:], xin[fb * 128:(fb + 1) * 128, ts], r=xr, w=[b])
                return a, b

            load_A(0)
            for qt in range(4):
                nx = load_y(qt, 0)
                for fb in range(16):
                    wr_, wm_, ga_, gb_ = nx
                    if fb + 1 < 16:
                        nx = load_y(qt, fb + 1)
                    for t2 in range(2):
                        cs2 = slice(t2 * TG, (t2 + 1) * TG)
                        yr, ym = Yr.next(), Ym.next()
                        for kc in range(16):
                            k.op("pe", lambda e: e.matmul(yr[:], wr_[:, kc, :], Ar[:, kc, cs2], start=(kc == 0), stop=(kc == 15)),
                                 r=[wr_, Ar], w=[yr], inc=(kc == 15))
                        for kc in range(16):
                            k.op("pe", lambda e: e.matmul(ym[:], wm_[:, kc, :], Am[:, kc, cs2], start=(kc == 0), stop=(kc == 15)),
                                 r=[wm_, Am], w=[ym], inc=(kc == 15))
                        a_, b_ = m1.next(), m2.next()
                        k.op("dve", lambda e: e.tensor_tensor(out=a_[:], in0=yr[:], in1=ga_[:, cs2], op=ALU.mult),
                             r=[yr, ga_], w=[a_])
                        k.op("dve", lambda e: e.tensor_tensor(out=b_[:], in0=ym[:], in1=gb_[:, cs2], op=ALU.mult),
                             r=[ym, gb_], w=[b_])
                        k.op("dve", lambda e: e.tensor_tensor(out=Mg[:, fb, cs2], in0=a_[:], in1=b_[:], op=ALU.add),
                             r=[a_, b_], w=[Mg])
                if qt + 1 < 4:
                    load_A(qt + 1)
                nx = load_o(qt, 0)
                for fb in range(16):
                    wo_, xo_ = nx
                    if fb + 1 < 16:
                        nx = load_o(qt, fb + 1)
                    for t2 in range(2):
                        cs2 = slice(t2 * TG, (t2 + 1) * TG)
                        yo = Yo.next()
                        for kc in range(16):
                            k.op("pe", lambda e: e.matmul(yo[:], wo_[:, kc, :], Mg[:, kc, cs2], start=(kc == 0), stop=(kc == 15)),
                                 r=[wo_, Mg], w=[yo], inc=(kc == 15))
                        x_ = xn.next()
                        gcol = self.modT[:, l * 48 + 32 + fb:l * 48 + 33 + fb]
                        k.op("dve", lambda e: e.scalar_tensor_tensor(out=x_[:], in0=yo[:], scalar=gcol, in1=xo_[:, cs2],
                                                                     op0=ALU.mult, op1=ALU.add),
                             r=[yo, self.modT, xo_], w=[x_])
                        t0 = qt * QT + t2 * TG
                        k.dma("sp", xout[fb * 128:(fb + 1) * 128, t0:t0 + TG], x_[:], r=[x_], w=[xout_trk], wplus=True)

    def phase_F(self, xin, xin_trk):
        nc, k = self.nc, self.k
        k.barrier()
        sb = k.sb
        xr = [xin_trk] if xin_trk is not None else []
        with ExitStack() as pa:
            X = Ring([sb(pa, "XF%d" % i, [128, 16, 256], F32, dma=True) for i in range(2)])
            OT = Ring([sb(pa, "OF%d" % i, [128, 16, 256], F32, dma=True) for i in range(2)])
            sq = Ring([sb(pa, "sqF%d" % i, [128, 256], F32) for i in range(3)])
            rs = Ring([sb(pa, "rsF%d" % i, [128, 256], F32) for i in range(2)])
            sps = Ring([k.ps(pa, "spsF%d" % i, [128, 256]) for i in range(2)])
            xv = xin.rearrange("(kc p) t -> p kc t", p=128)
            ov = self.outT.rearrange("(kc p) t -> p kc t", p=128)
            for g in range(16):
                Xg = X.next()
                k.dma("sp", Xg[:], xv[:, :, g * 256:(g + 1) * 256], r=xr, w=[Xg])
                sp_ = sps.next()
                for kc in range(16):
                    s = sq.next()
                    k.op("act", lambda e: e.activation(out=s[:], in_=Xg[:, kc, :], func=AF.Square), r=[Xg], w=[s])
                    k.op("pe", lambda e: e.matmul(sp_[:], self.ones_f[:], s[:], start=(kc == 0), stop=(kc == 15)),
                         r=[self.ones_f, s], w=[sp_], inc=True)
                r_ = rs.next()
                k.op("act", lambda e: e.activation(out=r_[:], in_=sp_[:], func=AF.Sqrt, bias=EPS, scale=1.0 / D),
                     r=[sp_], w=[r_])
                k.op("dve", lambda e: e.reciprocal(out=r_[:], in_=r_[:]), r=[r_], w=[r_])
                o_ = OT.next()
                for kc in range(16):
                    k.op("dve", lambda e: e.scalar_tensor_tensor(out=o_[:, kc, :], in0=Xg[:, kc, :], scalar=self.gfin[:, kc:kc + 1],
                                                                 in1=r_[:], op0=ALU.mult, op1=ALU.mult),
                         r=[Xg, self.gfin, r_], w=[o_])
                k.dma("sp", ov[:, :, g * 256:(g + 1) * 256], o_[:], r=[o_], w=[self.out_trk], wplus=True)


def prep_inputs(inputs):
    x = np.asarray(inputs["x"])
    B = x.shape[0]
    cv, rmask, ident = host_consts()
    w_uq = np.asarray(inputs["w_uq"])
    idx = np.concatenate([np.arange(h * 192 + 128, h * 192 + 192) for h in range(16)])
    w_uqr = np.ascontiguousarray(w_uq[:, :, idx])
    shared = {
        "w_mod": np.asarray(inputs["w_mod"]), "b_mod": np.asarray(inputs["b_mod"]),
        "g_norm": np.asarray(inputs["g_norm"]), "w_in": np.asarray(inputs["w_in"]),
        "g_cq": np.asarray(inputs["g_cq"]), "g_ckv": np.asarray(inputs["g_ckv"]),
        "w_uq": w_uq, "w_uqr": w_uqr, "w_ukv": np.asarray(inputs["w_ukv"]),
        "w_ret_proj": np.asarray(inputs["w_ret_proj"]), "w_mla_proj": np.asarray(inputs["w_mla_proj"]),
        "w_out": np.asarray(inputs["w_out"]), "g_final": np.asarray(inputs["g_final"]).reshape(1, D),
        "cvec_d": cv, "rmask_d": rmask, "ident_d": ident,
    }
    maps = []
    for core in range(8):
        b = core % B
        m = dict(shared)
        m["xT"] = np.ascontiguousarray(x[b].T)
        m["c"] = np.ascontiguousarray(np.asarray(inputs["c"])[b:b + 1])
        m["pos"] = np.ascontiguousarray(np.asarray(inputs["positions"])[b:b + 1]).astype(np.int32)
        maps.append(m)
    return maps


def kernel(**inputs):
    maps = prep_inputs(inputs)
    prog = Prog()
    nc = prog.build()
    B = np.asarray(inputs["x"]).shape[0]
    res = run_bass_kernel_spmd(nc, maps[:B], core_ids=list(range(B)))
    out = np.stack([np.ascontiguousarray(res.results[b]["outT"].T) for b in range(B)], axis=0)
    return out.astype(np.float32)
```
